# Optimizing a Trainium2 kernel written in Bass

```python
import math
import jax, jax.numpy as jnp
from jax import lax
import numpy as np

D_MODEL = 2048
BATCH = 4
SEQ = 4096
DEPTH = 1

CHUNK = 64
N_RET_HEADS = 8
RET_HEAD_DIM = D_MODEL // N_RET_HEADS
RET_V_HEAD_DIM = D_MODEL // N_RET_HEADS
D_RET = N_RET_HEADS * RET_HEAD_DIM
D_RET_V = N_RET_HEADS * RET_V_HEAD_DIM
POOL_WINDOWS = (2, 4, 8, 16)
POOL_GROUPS = len(POOL_WINDOWS)
D_POOL = D_MODEL // 2
POOL_GROUP_DIM = D_POOL // POOL_GROUPS
N_BRANCHES = 2
D_FF = ((8 * D_MODEL // 3 + 255) // 256) * 256
ROPE_BASE = 10000.0
NORM_EPS = 1e-6
PROJ_SIZES = (D_RET, D_RET, D_RET_V, D_RET_V, D_POOL, N_BRANCHES * D_MODEL)
D_PROJ = sum(PROJ_SIZES)

kernel_name = "hybrid_retention_pool_block"


def rms_norm(x, g):
    xf = x.astype(jnp.float32)
    y = xf * lax.rsqrt(jnp.mean(xf * xf, axis=-1, keepdims=True) + NORM_EPS)
    if g is not None:
        y = y * g.astype(jnp.float32)
    return y.astype(x.dtype)


def rotary(t, pos):
    dh = t.shape[-1]
    inv_freq = 1.0 / (ROPE_BASE ** (jnp.arange(0, dh, 2, dtype=jnp.float32) / dh))
    ang = pos.astype(jnp.float32)[:, None] * inv_freq[None, :]
    cos = jnp.cos(ang)[None, :, None, :]
    sin = jnp.sin(ang)[None, :, None, :]
    tf = t.astype(jnp.float32)
    t1, t2 = tf[..., : dh // 2], tf[..., dh // 2:]
    out = jnp.concatenate([t1 * cos - t2 * sin, t1 * sin + t2 * cos], axis=-1)
    return out.astype(t.dtype)


def retention_decays(dtype):
    log_g = jnp.log(1.0 - 2.0 ** (-5.0 - jnp.arange(N_RET_HEADS, dtype=jnp.float32)))
    n = jnp.arange(CHUNK, dtype=jnp.float32)
    dist = jnp.abs(n[:, None] - n[None, :])
    d_intra = jnp.exp(log_g[:, None, None] * dist[None])
    q_decay = jnp.exp(log_g[:, None] * (n[None, :] + 1.0))
    k_decay = jnp.exp(log_g[:, None] * (CHUNK - 1.0 - n[None, :]))
    chunk_decay = jnp.exp(log_g * CHUNK)
    return (d_intra.astype(dtype), q_decay.astype(dtype),
            k_decay.astype(dtype), chunk_decay.astype(dtype))


def chunkwise_retention(q, k, v):
    B, S, H, dk = q.shape
    dv = v.shape[-1]
    nc = S // CHUNK

    def to_chunks(t):
        d = t.shape[-1]
        return t.reshape(B, nc, CHUNK, H, d).transpose(1, 0, 3, 2, 4)

    qc, kc, vc = to_chunks(q), to_chunks(k), to_chunks(v)
    d_intra, q_decay, k_decay, chunk_decay = retention_decays(q.dtype)

    def step(state, xs):
        q_i, k_i, v_i = xs
        scores = jnp.einsum('bhnd,bhmd->bhnm', q_i, k_i) * d_intra[None]
        o_intra = jnp.einsum('bhnm,bhmv->bhnv', scores, v_i)
        o_cross = jnp.einsum('bhnd,bhdv->bhnv', q_i * q_decay[None, :, :, None], state)
        new_state = state * chunk_decay[None, :, None, None] + jnp.einsum(
            'bhmd,bhmv->bhdv', k_i * k_decay[None, :, :, None], v_i)
        return new_state, o_intra + o_cross

    state0 = jnp.zeros((B, H, dk, dv), dtype=q.dtype)
    _, o = lax.scan(step, state0, (qc, kc, vc))
    return o.transpose(1, 0, 3, 2, 4).reshape(B, S, H, dv)


def multiscale_causal_pool(p):
    B, S, G, Cg = p.shape
    pf = p.astype(jnp.float32)
    cs = jnp.concatenate([jnp.zeros((B, 1, G, Cg), jnp.float32),
                          jnp.cumsum(pf, axis=1)], axis=1)
    t = jnp.arange(S)
    outs = []
    for g, w in enumerate(POOL_WINDOWS):
        lo = jnp.maximum(t + 1 - w, 0)
        cnt = (t + 1 - lo).astype(jnp.float32)
        win_sum = cs[:, t + 1, g, :] - cs[:, lo, g, :]
        outs.append(win_sum / cnt[None, :, None])
    pooled = jnp.stack(outs, axis=2) - pf
    return pooled.astype(p.dtype)


def setup_inputs(seed: int = 0) -> dict:
    key = jax.random.key(seed)
    ks = jax.random.split(key, 13)
    f32 = jnp.float32

    def dense(k, shape, fan_in):
        return jax.random.normal(k, shape, f32) * (fan_in ** -0.5)

    def gain(k, shape):
        return 1.0 + 0.02 * jax.random.normal(k, shape, f32)

    return {
        "x": jax.random.normal(ks[0], (BATCH, SEQ, D_MODEL), f32),
        "norm1_g": gain(ks[1], (DEPTH, D_MODEL)),
        "w_in": dense(ks[2], (DEPTH, D_MODEL, D_PROJ), D_MODEL),
        "w_ret_branch": dense(ks[3], (DEPTH, D_RET_V, D_MODEL), D_RET_V),
        "w_pool_group": dense(ks[4], (DEPTH, POOL_GROUPS, POOL_GROUP_DIM, POOL_GROUP_DIM), POOL_GROUP_DIM),
        "pool_scale": gain(ks[5], (DEPTH, D_POOL)),
        "w_pool_branch": dense(ks[6], (DEPTH, D_POOL, D_MODEL), D_POOL),
        "w_out": dense(ks[7], (DEPTH, D_MODEL, D_MODEL), D_MODEL),
        "norm2_g": gain(ks[8], (DEPTH, D_MODEL)),
        "w_ffn_in": dense(ks[9], (DEPTH, D_MODEL, 2 * D_FF), D_MODEL),
        "w_ffn_out": dense(ks[10], (DEPTH, D_FF, D_MODEL), D_FF),
        "norm_final_g": gain(ks[11], (D_MODEL,)),
    }


def reference(x, norm1_g, w_in, w_ret_branch, w_pool_group, pool_scale, w_pool_branch,
              w_out, norm2_g, w_ffn_in, w_ffn_out, norm_final_g):
    B, S, _ = x.shape
    pos = jnp.arange(S)
    split_idx = list(np.cumsum(PROJ_SIZES)[:-1])
    scale = RET_HEAD_DIM ** -0.5
    h = x
    for l in range(DEPTH):
        u = rms_norm(h, norm1_g[l])
        proj = u @ w_in[l]
        q, k, v, rg, pz, gates = jnp.split(proj, split_idx, axis=-1)

        q = rotary(q.reshape(B, S, N_RET_HEADS, RET_HEAD_DIM), pos) * scale
        k = rotary(k.reshape(B, S, N_RET_HEADS, RET_HEAD_DIM), pos) * scale
        v = v.reshape(B, S, N_RET_HEADS, RET_V_HEAD_DIM)
        o = chunkwise_retention(q, k, v)
        o = rms_norm(o, None).reshape(B, S, D_RET_V)
        y_ret = (o * jax.nn.silu(rg)) @ w_ret_branch[l]

        pz = pz.reshape(B, S, POOL_GROUPS, POOL_GROUP_DIM)
        pooled = multiscale_causal_pool(pz)
        pooled = jnp.einsum('bsgc,gcd->bsgd', pooled, w_pool_group[l]).reshape(B, S, D_POOL)
        y_pool = (pooled * pool_scale[l]) @ w_pool_branch[l]

        g_ret, g_pool = jnp.split(gates, 2, axis=-1)
        merged = jax.nn.sigmoid(g_ret) * y_ret + jax.nn.sigmoid(g_pool) * y_pool
        h = h + merged @ w_out[l]

        u2 = rms_norm(h, norm2_g[l])
        a, b = jnp.split(u2 @ w_ffn_in[l], 2, axis=-1)
        h = h + (jax.nn.silu(a) * b) @ w_ffn_out[l]
    return rms_norm(h, norm_final_g)
```

```python
import numpy as np
from contextlib import ExitStack
import concourse.bass as bass
import concourse.mybir as mybir
from concourse.bass_utils import run_bass_kernel_spmd

F32 = mybir.dt.float32
BF16 = mybir.dt.bfloat16
ALU = mybir.AluOpType
AF = mybir.ActivationFunctionType

D = 2048
KC = 16
TT = 512
NSUB = 4
NTOK = 2048
NT = NTOK // TT
H = 8
DFF = 5632
NJ = DFF // 128
EPS = 1e-6
C_Q, C_K, C_V, C_RG, C_PZ, C_GR, C_GP = 0, 2048, 4096, 6144, 8192, 9216, 11264
NSLOT = 6
ENGS = ("pe", "act", "dve", "pool", "sp")


class Op:
    __slots__ = ("eng", "fn", "deps", "sig", "cnt", "dma_sem", "dma_val", "idx")

    def __init__(self, eng, fn, dma_sem=None):
        self.eng, self.fn, self.dma_sem = eng, fn, dma_sem
        self.deps, self.sig, self.cnt, self.dma_val, self.idx = [], False, 0, 0, 0


class Prog:
    def __init__(self):
        self.ops = {e: [] for e in ENGS}
        self.wr, self.rd = {}, {}
        self.dma_cnt = {}
        self.n = 0

    def _add(self, op, reads, writes):
        deps = {}

        def need(x, raw):
            if x is None:
                return
            if x.dma_sem is None:
                if x.eng == op.eng and op.dma_sem is None:
                    if op.eng == "pe":
                        return
                k = x.eng
            else:
                k = ("dma", x.dma_sem)
            if k not in deps or deps[k].idx < x.idx:
                deps[k] = x

        for r in reads:
            need(self.wr.get(r), True)
        for w in writes:
            need(self.wr.get(w), False)
            for x in self.rd.get(w, {}).values():
                need(x, False)
        op.deps = []
        for x in deps.values():
            if x.dma_sem is None:
                x.sig = True
                op.deps.append((x, None, None))
            else:
                op.deps.append((None, x.dma_sem, self.dma_cnt[x.dma_sem]))
        me = op.eng if op.dma_sem is None else ("dma", op.dma_sem)
        for r in reads:
            self.rd.setdefault(r, {})[me] = op
        for w in writes:
            self.wr[w] = op
            self.rd[w] = {}
        op.idx = self.n
        self.n += 1
        self.ops[op.eng].append(op)
        return op

    def op(self, eng, fn, reads=(), writes=()):
        return self._add(Op(eng, fn), list(reads), list(writes))

    def dma(self, eng, fn, sem, reads=(), writes=()):
        o = Op(eng, fn, dma_sem=sem)
        self.dma_cnt.setdefault(sem, 0)
        self._add(o, list(reads), list(writes))
        self.dma_cnt[sem] += 16
        o.dma_val = self.dma_cnt[sem]
        return o

    def emit_engine(self, e, handle, sems, lookahead=0):
        known = {}
        ops = self.ops[e]
        for i, op in enumerate(ops):
            for (x, dsem, dval) in op.deps:
                if x is not None:
                    name, val = x.eng, x.cnt
                else:
                    name, val = dsem, dval
                if known.get(name, 0) >= val:
                    continue
                if x is not None and lookahead:
                    for op2 in ops[i + 1:i + 1 + lookahead]:
                        for (x2, _, _) in op2.deps:
                            if x2 is not None and x2.eng == name and x2.idx < op.idx and x2.cnt > val:
                                val = x2.cnt
                handle.wait_ge(sems[name], val)
                known[name] = val
            inst = op.fn(handle)
            if op.dma_sem is not None:
                inst.then_inc(sems[op.dma_sem], 16)
            elif op.sig:
                inst.then_inc(sems[e], 1)

    def finalize(self):
        for e in ENGS:
            c = 0
            for op in self.ops[e]:
                if op.dma_sem is None and op.sig:
                    c += 1
                    op.cnt = c


def merge_streams(main, bg):
    acc = 0
    while True:
        try:
            w = next(main)
        except StopIteration:
            break
        acc += 2 if w is None else w
        while acc >= 2 and bg is not None:
            acc -= 2
            try:
                next(bg)
            except StopIteration:
                bg = None
    if bg is not None:
        for _ in bg:
            pass


def drain(g):
    if g is not None:
        for _ in g:
            pass


def build_nc(debug_taps=False):
    nc = bass.Bass("TRN2", target_bir_lowering=False)
    P = Prog()

    def din(name, shape):
        return nc.dram_tensor(name, list(shape), F32, kind="ExternalInput").ap()

    xm = din("xm", [NTOK, D])
    xp = din("xp", [NTOK, D])
    w_in = din("w_in", [D, 13312]).rearrange("(kc p) c -> p kc c", p=128)
    w_rb = din("w_rb", [D, D]).rearrange("(kc p) c -> p kc c", p=128)
    w_pg = din("w_pg", [1024, 256]).rearrange("(kc p) c -> p kc c", p=128)
    w_pb = din("w_pb", [1024, D]).rearrange("(kc p) c -> p kc c", p=128)
    w_o = din("w_o", [D, D]).rearrange("(kc p) c -> p kc c", p=128)
    w_fi = din("w_fi", [D, 2 * DFF]).rearrange("(kc p) c -> p kc c", p=128)
    w_fo = din("w_fo", [DFF, D]).rearrange("(kc p) c -> p kc c", p=128)
    c_small = din("c_small", [128, 64])
    c_gf = din("c_gf", [128, D])
    c_mask = din("c_mask", [128, H * 128])
    c_ident = din("c_ident", [128, 128])
    c_ag = din("c_ag", [128, 12 * 128])
    c_rotm = din("c_rotm", [128, 2, NTOK])
    c_rotp = din("c_rotp", [128, 2, NTOK])
    out_d = nc.dram_tensor("out", [NTOK, D], F32, kind="ExternalOutput").ap()

    st = ExitStack()

    def sb(name, shape, dt):
        return st.enter_context(nc.sbuf_tensor(name, list(shape), dt))[:]

    ident32 = sb("ident32", [128, 128], F32)
    identbf = sb("identbf", [128, 128], BF16)
    csm = sb("csm", [128, 64], F32)
    maskT = sb("maskT", [128, H, 128], F32)
    AgT = sb("AgT", [128, 12, 128], BF16)
    rot = sb("rot", [128, 2, TT], F32)
    XM = sb("XM", [128, 2 * D], F32)
    XN = sb("XN", [128, D], F32)
    junk = sb("junk", [128, D], BF16)
    uT = sb("uT", [128, KC, TT], BF16)
    st32 = sb("st32", [128, H, 512], F32)
    stbf = sb("stbf", [128, H, 512], BF16)
    rtmp1 = sb("rtmp1", [128, TT], F32)
    rtmp2 = sb("rtmp2", [128, TT], F32)
    sgA = sb("sgA", [128, TT], F32)
    sgB = sb("sgB", [128, TT], F32)
    tprod = sb("tprod", [128, TT], F32)
    R2 = sb("R2", [128, 12288], BF16)
    R3 = sb("R3", [128, 8192], F32)
    pzhalo = sb("pzhalo", [128, 1024], BF16)
    gfbc = sb("gfbc", [128, D], F32)
    small = sb("small", [128, 64], F32)
    ring = [sb(f"ring{i}", [128, 4096], BF16) for i in range(NSLOT)]
    ps = [st.enter_context(nc.psum_tensor(f"ps{i}", [128, 512], F32))[:] for i in range(8)]

    g1c = csm[:, 0:16]
    g2c = csm[:, 16:32]
    psc = csm[:, 32:40]
    kdec = csm[:, 40:48]
    epsh = csm[:, 48:56]

    xin = [XM[:, 0:D], XM[:, D:2 * D]]
    mergedT = XM.bitcast(BF16).rearrange("p (a b) -> p a b", a=KC)
    ypt = XN.rearrange("p (a b) -> p a b", a=4)
    ogT = R2[:, 0:8192].rearrange("p (a b) -> p a b", a=KC)
    pooled2T = R2[:, 8192:12288].rearrange("p (a b) -> p a b", a=8)
    hffT = R2[:, 0:11264].rearrange("p (a b) -> p a b", a=22)
    hbuf = R3.rearrange("p (a b) -> p a b", a=NSUB)
    R3b = R3.bitcast(BF16)

    def r3keys(off, nbytes):
        return [("R3", i) for i in range(off // 2048, (off + nbytes + 2047) // 2048)]

    def r3_bf(off, shape):
        n = int(np.prod(shape))
        ap = R3b[:, off // 2: off // 2 + n]
        if len(shape) == 2:
            ap = ap.rearrange("p (a b) -> p a b", a=shape[0])
        return ap, r3keys(off, n * 2)

    def r3_f32(off, shape):
        n = int(np.prod(shape))
        ap = R3[:, off // 4: off // 4 + n]
        if len(shape) == 2:
            ap = ap.rearrange("p (a b) -> p a b", a=shape[0])
        return ap, r3keys(off, n * 4)

    K = 1024
    qT = [r3_bf(0, [2, TT]), r3_bf(2 * K, [2, TT])]
    kT = [r3_bf(4 * K, [2, TT]), r3_bf(6 * K, [2, TT])]
    kd = [r3_bf(8 * K, [NSUB, 256]), r3_bf(10 * K, [NSUB, 256])]
    vv = [r3_bf(12 * K, [NSUB, 256]), r3_bf(14 * K, [NSUB, 256])]
    og = [r3_bf(16 * K, [NSUB, 256]), r3_bf(18 * K, [NSUB, 256])]
    srg = [r3_f32(20 * K, [NSUB, 256]), r3_f32(24 * K, [NSUB, 256])]
    sT = [r3_bf(28 * K, [128]), r3_bf(30 * K, [128])]
    def r2keys(off, nbytes):
        return [("R2", i) for i in range(off // 2048, (off + nbytes + 2047) // 2048)]
    pz_tm = (R2[:, 0:4096].rearrange("p (a b) -> p a b", a=NSUB), r2keys(0, 8192))
    pooledT = (R2[:, 4096:8192].rearrange("p (a b) -> p a b", a=8), r2keys(8192, 8192))
    OGK = lambda h: [("R2", h)]
    OGK_ALL = [("R2", i) for i in range(8)]
    P2K = r2keys(16384, 8192)
    HFK = lambda jl: [("R2", jl // 2)]
    HK = lambda sub: r3keys(sub * 8192, 8192)

    _sc = [0]

    def scol(n=1):
        c = _sc[0]
        _sc[0] = (c + 1) % 64
        return small[:, c:c + 1], ("small", c)

    _bk = [0]

    _bq = [0]
    _bg = [0]

    def bank(cls=None):
        if cls == "qk":
            b = _bq[0]
            _bq[0] = (b + 1) % 4
            return b
        if cls == "gen":
            b = 4 + _bg[0]
            _bg[0] = (_bg[0] + 1) % 4
            return b
        b = _bk[0]
        _bk[0] = (b + 1) % 8
        return b

    deferred = []

    def flush_deferred(n=None):
        k = len(deferred) if n is None else min(n, len(deferred))
        for _ in range(k):
            deferred.pop(0)()

    units = []
    wstate = {"next_issue": 0, "next_acq": 0, "released": 0}

    def slot3(i, a, b):
        return ring[i].rearrange("p (a b) -> p a b", a=a)

    def w_issue(u):
        name, parts = units[u]
        s = u % NSLOT
        for (dst_fn, src) in parts:
            dst = dst_fn(s)
            P.dma("pool", (lambda d, s_: (lambda e: e.dma_start(out=d, in_=s_)))(dst, src),
                  sem=f"w{s}", writes=[("w", s)])

    def w_pump():
        while wstate["next_issue"] < len(units) and wstate["next_issue"] - NSLOT < wstate["released"]:
            w_issue(wstate["next_issue"])
            wstate["next_issue"] += 1

    def w_acquire(name):
        u = wstate["next_acq"]
        assert units[u][0] == name, (units[u][0], name)
        wstate["next_acq"] += 1
        w_pump()
        assert u < wstate["next_issue"]
        return u % NSLOT

    def w_release(s):
        assert wstate["released"] % NSLOT == s
        wstate["released"] += 1
        w_pump()

    def col_unit(name, w, c0, ncols, nk=KC, k0=0):
        parts = [((lambda s, nk=nk, ncols=ncols:
                   ring[s][:, 0:nk * ncols].rearrange("p (a b) -> p a b", a=nk)),
                  w[:, k0:k0 + nk, c0:c0 + ncols])]
        units.append((name, parts))

    def col_unit2(name, w, ca, cb_, nc_each):
        def dst(off):
            return lambda s: ring[s][:, 0:KC * 2 * nc_each].rearrange(
                "p (a b) -> p a b", a=KC)[:, :, off:off + nc_each]
        units.append((name, [(dst(0), w[:, :, ca:ca + nc_each]),
                             (dst(nc_each), w[:, :, cb_:cb_ + nc_each])]))

    def wview(s, nk, ncols):
        return ring[s][:, 0:nk * ncols].rearrange("p (a b) -> p a b", a=nk)

    for t in range(NT):
        for h in range(H):
            col_unit(f"pk{t}_{h}", w_in, C_K + h * 256, 256)
            col_unit(f"pv{t}_{h}", w_in, C_V + h * 256, 256)
        if t == NT - 1:
            for c4 in range(4):
                col_unit(f"ppz{c4}", w_in, C_PZ + c4 * 256, 256)
    for t in range(NT):
        for c4 in range(4):
            col_unit(f"pz{c4}_{t}", w_in, C_PZ + c4 * 256, 256)
        col_unit(f"wpg_{t}", w_pg, 0, 256, nk=8)
        for h in range(H):
            col_unit(f"q{t}_{h}", w_in, C_Q + h * 256, 256)
            col_unit(f"k{t}_{h}", w_in, C_K + h * 256, 256)
            col_unit(f"v{t}_{h}", w_in, C_V + h * 256, 256)
            col_unit(f"rg{t}_{h}", w_in, C_RG + h * 256, 256)
        def _u12(cb):
            col_unit(f"wpb{t}_{cb}", w_pb, cb * 256, 256, nk=8)
            col_unit(f"gp{t}_{cb}", w_in, C_GP + cb * 256, 256)
        _u12(0)
        for cb in range(8):
            if cb + 1 < 8:
                _u12(cb + 1)
            col_unit(f"wrb{t}_{cb}", w_rb, cb * 256, 256)
            col_unit(f"gr{t}_{cb}", w_in, C_GR + cb * 256, 256)
        for cb in range(8):
            col_unit(f"wo{t}_{cb}", w_o, cb * 256, 256)
        for half in range(2):
            for jj in range(11):
                j0 = half * 22 + jj * 2
                col_unit(f"fia{t}_{half}_{jj}", w_fi, j0 * 128, 256)
                col_unit(f"fib{t}_{half}_{jj}", w_fi, DFF + j0 * 128, 256)
            for cb in range(8):
                col_unit(f"foa{t}_{half}_{cb}", w_fo, cb * 256, 256, nk=16, k0=half * 22)
                col_unit(f"fob{t}_{half}_{cb}", w_fo, cb * 256, 256, nk=6, k0=half * 22 + 16)

    def MM(out, lhsT, rhs, start, stop, reads, bk):
        P.op("pe", lambda e: e.matmul(out, lhsT, rhs, start=start, stop=stop),
             reads=reads, writes=[("ps", bk)])

    def TR(out, in_, ident, reads, bk):
        P.op("pe", lambda e: e.transpose(out, in_, ident), reads=reads, writes=[("ps", bk)])

    def ACT(out, in_, func, reads, writes, scale=1.0, bias=0.0, accum=None):
        if accum is None:
            P.op("act", lambda e: e.activation(out, in_, func, bias=bias, scale=scale),
                 reads=reads, writes=writes)
        else:
            P.op("act", lambda e: e.activation(out, in_, func, bias=bias, scale=scale,
                                               accum_out=accum), reads=reads, writes=writes)

    def TT_(out, a, b, op, reads, writes, eng="dve"):
        P.op(eng, lambda e: e.tensor_tensor(out, a, b, op), reads=reads, writes=writes)

    def STT(out, in0, scalar, in1, op0, op1, reads, writes):
        P.op("dve", lambda e: e.scalar_tensor_tensor(out, in0, scalar, in1, op0, op1),
             reads=reads, writes=writes)

    def TS(out, in0, s1, op0, reads, writes, s2=None, op1=None):
        if op1 is None:
            P.op("dve", lambda e: e.tensor_scalar(out, in0, s1, None, op0), reads=reads, writes=writes)
        else:
            P.op("dve", lambda e: e.tensor_scalar(out, in0, s1, s2, op0, op1), reads=reads, writes=writes)

    def COPYV(out, in_, reads, writes):
        P.op("dve", lambda e: e.tensor_copy(out, in_), reads=reads, writes=writes)

    def RECIP(out, in_, reads, writes):
        P.op("dve", lambda e: e.reciprocal(out, in_), reads=reads, writes=writes)

    def SPDMA(out, in_, sem, reads, writes):
        P.dma("sp", lambda e: e.dma_start(out=out, in_=in_), sem=sem, reads=reads, writes=writes)

    SPDMA(ident32, c_ident, "c0", [], [("ident32",)])
    SPDMA(csm, c_small, "c0", [], [("csm",)])
    SPDMA(maskT.rearrange("p a b -> p (a b)"), c_mask, "c0", [], [("maskT",)])
    SPDMA(gfbc, c_gf, "c0", [], [("gfbc",)])
    P.dma("pool", lambda e: e.dma_start(out=identbf, in_=c_ident), sem="c1", writes=[("identbf",)])
    P.dma("pool", lambda e: e.dma_start(out=AgT.rearrange("p a b -> p (a b)"), in_=c_ag),
          sem="c1", writes=[("AgT",)])
    P.op("dve", lambda e: e.memset(st32.rearrange("p a b -> p (a b)"), 0.0),
         writes=[("st32", h) for h in range(H)])
    P.op("dve", lambda e: e.memset(stbf.rearrange("p a b -> p (a b)"), 0.0),
         writes=[("stbf", h) for h in range(H)])

    xcnt = [0]

    uT_alt = R2[:, 0:8192].rearrange("p (a b) -> p a b", a=KC)
    UTB = [(uT, lambda sub: [("uT", sub)], [("uT", s_) for s_ in range(NSUB)]),
           (uT_alt, lambda sub: [("R2", i) for i in range(8)], [("R2", i) for i in range(8)])]

    def rms_to_T(src_ap_fn, src_reads_fn, gcols, n_in, load_fn=None, ub=0, inplace=False):
        uTd, ukey_fn, _ = UTB[ub]

        def norm(sub):
            if load_fn is not None:
                load_fn(sub)
            src = src_ap_fn(sub)
            sreads = src_reads_fn(sub)
            ss, kss = scol()
            rs, krs = scol()
            ri, kri = scol()
            ACT(junk, src, AF.Square, sreads, [("junk",), kss], accum=ss)
            ACT(rs, ss, AF.Sqrt, [kss], [krs], scale=1.0 / n_in, bias=EPS)
            RECIP(ri, rs, [krs], [kri])
            if inplace:
                TS(src, src, ri, ALU.mult, sreads + [kri], sreads)
                return src, sreads
            TS(XN, src, ri, ALU.mult, sreads + [kri], XNK)
            return XN, XNK

        def trans(sub, xn, xkeys):
            for q in range(4):
                bk = bank()
                for i in range(4):
                    kc = q * 4 + i
                    TR(ps[bk][:, i * 128:(i + 1) * 128], xn[:, kc * 128:(kc + 1) * 128], ident32,
                       xkeys + [("ident32",)], bk)
                gb = gcols[:, q * 4:q * 4 + 4].unsqueeze(2).to_broadcast([128, 4, 128])
                TT_(uTd[:, q * 4:q * 4 + 4, sub * 128:(sub + 1) * 128],
                    ps[bk].rearrange("p (a b) -> p a b", a=4), gb, ALU.mult,
                    [("csm",)], [("ps", bk)] + ukey_fn(sub))

        if not inplace:
            for sub in range(NSUB):
                xn, xk = norm(sub)
                trans(sub, xn, xk)
                yield
            return
        cur = norm(0)
        yield
        for sub in range(NSUB):
            nxt_ = norm(sub + 1) if sub + 1 < NSUB else None
            trans(sub, *cur)
            cur = nxt_
            yield

    def stage_A(xsrc, t, ub=0):
        def load(sub):
            i = xcnt[0] % 2
            xcnt[0] += 1
            load.cur = i
            SPDMA(xin[i], xsrc[t * TT + sub * 128: t * TT + (sub + 1) * 128, :], f"x{i}", [], [("XM", i)])
        return rms_to_T(lambda sub: xin[load.cur], lambda sub: [("XM", load.cur)], g1c, D, load_fn=load,
                        ub=ub, inplace=True)

    UT_ALL = [("uT", s) for s in range(NSUB)]
    XNK = [("XN", i) for i in range(4)]

    def load_rot(src, t):
        SPDMA(rot, src[:, :, t * TT:(t + 1) * TT], "rot", [], [("rot",)])

    def rotary_defer(bA, bB, dst, dkeys):
        cos, sin = rot[:, 0, :], rot[:, 1, :]
        deferred.append(lambda: TT_(rtmp1, ps[bA], cos, ALU.mult, [("rot",)], [("ps", bA), ("rt1",)]))
        deferred.append(lambda: TT_(rtmp2, ps[bB], sin, ALU.mult, [("rot",)], [("ps", bB), ("rt2",)]))
        deferred.append(lambda: TT_(dst[:, 0, :], rtmp1, rtmp2, ALU.subtract, [("rt1",), ("rt2",)], dkeys))
        deferred.append(lambda: TT_(rtmp1, ps[bA], sin, ALU.mult, [("rot",)], [("ps", bA), ("rt1",)]))
        deferred.append(lambda: TT_(rtmp2, ps[bB], cos, ALU.mult, [("rot",)], [("ps", bB), ("rt2",)]))
        deferred.append(lambda: TT_(dst[:, 1, :], rtmp1, rtmp2, ALU.add, [("rt1",), ("rt2",)], dkeys))

    def proj_rot_steps(s, w, coff, dst, dkeys, ub=0):
        uTs, ukey_fn, ukeys_all = UTB[ub]
        bA, bB = bank("qk"), bank("qk")
        for dblk, bk in ((0, bA), (1, bB)):
            for kc in range(KC):
                MM(ps[bk], w[:, kc, coff + dblk * 128: coff + (dblk + 1) * 128], uTs[:, kc, :],
                   kc == 0, kc == KC - 1, [("w", s)] + ukeys_all, bk)
            if dblk == 1:
                rotary_defer(bA, bB, dst, dkeys)
            flush_deferred(2)
            yield

    def v_like_steps(name, dst, dkeys, func, nflush, ub=0):
        uTs, ukey_fn, ukeys_all = UTB[ub]
        s = w_acquire(name)
        w = wview(s, KC, 256)
        for sub in range(NSUB):
            bk = bank("gen")
            for kc in range(KC):
                MM(ps[bk][:, 0:256], uTs[:, kc, sub * 128:(sub + 1) * 128], w[:, kc, :],
                   kc == 0, kc == KC - 1, [("w", s)] + ukey_fn(sub), bk)
            ACT(dst[:, sub, :], ps[bk][:, 0:256], func, [], [("ps", bk)] + dkeys)
            flush_deferred(nflush)
            yield 1
        w_release(s)

    def gen_proj_prefix(t, h):
        hp = h % 2
        s = w_acquire(f"pk{t}_{h}")
        yield from proj_rot_steps(s, wview(s, KC, 256), 0, *kT[hp], ub=t % 2)
        w_release(s)
        yield from v_like_steps(f"pv{t}_{h}", vv[hp][0], vv[hp][1], AF.Copy, 2, ub=t % 2)
        flush_deferred()

    def gen_proj(t, h):
        hp = h % 2
        s = w_acquire(f"q{t}_{h}")
        yield from proj_rot_steps(s, wview(s, KC, 256), 0, *qT[hp])
        w_release(s)
        s = w_acquire(f"k{t}_{h}")
        yield from proj_rot_steps(s, wview(s, KC, 256), 0, *kT[hp])
        w_release(s)
        yield from v_like_steps(f"v{t}_{h}", vv[hp][0], vv[hp][1], AF.Copy, 1)
        yield from v_like_steps(f"rg{t}_{h}", srg[hp][0], srg[hp][1], AF.Silu, 1)
        flush_deferred()

    def gen_ret(t, h, prefix, last_prefix_tile=False):
        hp = h % 2
        g = 1.0 - 2.0 ** (-5.0 - h)
        g128 = float(np.float64(g) ** 128)
        kTa, kTk = kT[hp]
        qTa, qTk = qT[hp]
        kda, kdk = kd[hp]
        va, vk = vv[hp]
        oga, ogk = og[hp]
        sa, sk = srg[hp]
        bk = bank("gen")
        bv = ps[bk].bitcast(BF16).rearrange("p (a b) -> p a b", a=NSUB)
        for sub in range(NSUB):
            for dblk in range(2):
                TR(bv[:, sub, dblk * 128:(dblk + 1) * 128], kTa[:, dblk, sub * 128:(sub + 1) * 128],
                   identbf, kTk + [("identbf",)], bk)
        ACT(kda.rearrange("p a b -> p (a b)"), ps[bk].bitcast(BF16), AF.Copy, [("csm",)],
            [("ps", bk)] + kdk, scale=kdec[:, h:h + 1])
        yield

        def phase_A(sub):
            tk = slice(sub * 128, (sub + 1) * 128)
            sTa, sTk = sT[sub % 2]
            bS = bank("gen")
            for dblk in range(2):
                MM(ps[bS][:, 0:128], kTa[:, dblk, tk], qTa[:, dblk, tk], dblk == 0, dblk == 1,
                   kTk + qTk, bS)
            TT_(sTa, ps[bS][:, 0:128], maskT[:, h, :], ALU.mult, [("maskT",)], [("ps", bS)] + sTk)

        def phase_S(sub, copy_bf):
            bSt = bank("gen")
            for dblk in range(2):
                MM(ps[bSt][:, dblk * 256:(dblk + 1) * 256], kda[:, sub, dblk * 128:(dblk + 1) * 128],
                   va[:, sub, :], True, True, kdk + vk, bSt)
            STT(st32[:, h, :], st32[:, h, :], g128, ps[bSt], ALU.mult, ALU.add,
                [("st32", h)], [("ps", bSt), ("st32", h)])
            if copy_bf:
                ACT(stbf[:, h, :], st32[:, h, :], AF.Copy, [("st32", h)], [("stbf", h)])

        def phase_B(sub):
            tk = slice(sub * 128, (sub + 1) * 128)
            sTa, sTk = sT[sub % 2]
            bO = bank("gen")
            MM(ps[bO][:, 0:256], sTa, va[:, sub, :], True, False, sTk + vk, bO)
            for dblk in range(2):
                MM(ps[bO][:, 0:256], qTa[:, dblk, tk], stbf[:, h, dblk * 256:(dblk + 1) * 256],
                   False, dblk == 1, qTk + [("stbf", h)], bO)
            phase_S(sub, True)
            ss, kss = scol()
            rs, krs = scol()
            ri, kri = scol()
            ACT(junk[:, 0:256], ps[bO][:, 0:256], AF.Square, [], [("ps", bO), ("junk",), kss], accum=ss)
            ACT(rs, ss, AF.Sqrt, [kss, ("csm",)], [krs], scale=1.0 / 256, bias=epsh[:, h:h + 1])
            RECIP(ri, rs, [krs], [kri])
            STT(oga[:, sub, :], ps[bO][:, 0:256], ri, sa[:, sub, :], ALU.mult, ALU.mult,
                [kri] + sk, [("ps", bO)] + ogk)

        if prefix:
            for sub in range(NSUB):
                phase_S(sub, last_prefix_tile and sub == NSUB - 1)
                if sub % 2 == 1:
                    yield
            return
        phase_A(0)
        yield
        for sub in range(NSUB):
            phase_B(sub)
            if sub + 1 < NSUB:
                phase_A(sub + 1)
            yield
        bk = bank("gen")
        bv = ps[bk].bitcast(BF16).rearrange("p (a b) -> p a b", a=2)
        for sub in range(NSUB):
            for dvb in range(2):
                TR(bv[:, dvb, sub * 128:(sub + 1) * 128], oga[:, sub, dvb * 128:(dvb + 1) * 128],
                   identbf, ogk + [("identbf",)], bk)
        ACT(ogT[:, 2 * h:2 * h + 2, :].rearrange("p a b -> p (a b)"), ps[bk].bitcast(BF16), AF.Copy,
            [], [("ps", bk)] + OGK(h))
        yield

    def pz_proj(t, names, dst_fn, dkeys, subs, ub=0):
        uTs, ukey_fn, ukeys_all = UTB[ub]
        for c4, name in enumerate(names):
            s = w_acquire(name)
            w = wview(s, KC, 256)
            for sub in subs:
                bk = bank()
                for kc in range(KC):
                    MM(ps[bk][:, 0:256], uTs[:, kc, sub * 128:(sub + 1) * 128], w[:, kc, :],
                       kc == 0, kc == KC - 1, [("w", s)] + ukey_fn(sub), bk)
                ACT(dst_fn(sub)[:, c4 * 256:(c4 + 1) * 256], ps[bk][:, 0:256], AF.Copy, [],
                    [("ps", bk)] + dkeys)
            w_release(s)

    def pool_chain(t):
        pza, pzk = pz_tm
        pla, plk = pooledT
        pz_proj(t, [f"pz{c4}_{t}" for c4 in range(4)], lambda sub: pza[:, sub, :], pzk, range(NSUB))
        for sub in range(NSUB):
            for half in range(2):
                bk = bank()
                for i in range(4):
                    cblk = half * 4 + i
                    gidx = cblk // 2
                    a_main = AgT[:, (8 + gidx) if (t == 0 and sub == 0) else gidx, :]
                    a_halo = AgT[:, 4 + gidx, :]
                    cs = slice(cblk * 128, (cblk + 1) * 128)
                    o = ps[bk][:, i * 128:(i + 1) * 128]
                    MM(o, pza[:, sub, cs], a_main, True, False, pzk + [("AgT",)], bk)
                    if sub == 0:
                        MM(o, pzhalo[:, cs], a_halo, False, True, [("pzhalo",), ("AgT",)], bk)
                    else:
                        MM(o, pza[:, sub - 1, cs], a_halo, False, True, pzk + [("AgT",)], bk)
                COPYV(pla[:, half * 4:half * 4 + 4, sub * 128:(sub + 1) * 128],
                      ps[bk].rearrange("p (a b) -> p a b", a=4), [], [("ps", bk)] + plk)
        COPYV(pzhalo, pza[:, NSUB - 1, :], pzk, [("pzhalo",)])
        s = w_acquire(f"wpg_{t}")
        w = wview(s, 8, 256)
        for dblk in range(8):
            gidx, dh = dblk // 2, dblk % 2
            bk = bank()
            for ch in range(2):
                MM(ps[bk], w[:, gidx * 2 + ch, dh * 128:(dh + 1) * 128], pla[:, gidx * 2 + ch, :],
                   ch == 0, ch == 1, [("w", s)] + plk, bk)
            TS(pooled2T[:, dblk, :], ps[bk], psc[:, dblk:dblk + 1], ALU.mult, [("csm",)],
               [("ps", bk)] + P2K)
        w_release(s)

    def gen_merge(t, bg_holder):
        def steps12(cb):
            yb = (cb % 2) * 2
            s = w_acquire(f"wpb{t}_{cb}")
            w = wview(s, 8, 256)
            for blk in range(2):
                bk = bank()
                for kc in range(8):
                    MM(ps[bk], w[:, kc, blk * 128:(blk + 1) * 128], pooled2T[:, kc, :],
                       kc == 0, kc == 7, [("w", s)] + P2K, bk)
                ACT(ypt[:, yb + blk, :], ps[bk], AF.Copy, [], [("ps", bk), ("XN", yb + blk)])
                yield
            w_release(s)
            s = w_acquire(f"gp{t}_{cb}")
            w = wview(s, KC, 256)
            for blk in range(2):
                bk = bank()
                for kc in range(KC):
                    MM(ps[bk], w[:, kc, blk * 128:(blk + 1) * 128], uT[:, kc, :],
                       kc == 0, kc == KC - 1, [("w", s)] + UT_ALL, bk)
                ACT(sgA, ps[bk], AF.Sigmoid, [], [("ps", bk), ("sgA",)])
                TT_(ypt[:, yb + blk, :], ypt[:, yb + blk, :], sgA, ALU.mult,
                    [("XN", yb + blk), ("sgA",)], [("XN", yb + blk)])
                yield
            w_release(s)

        def step3(cb):
            yb = (cb % 2) * 2
            s1 = w_acquire(f"wrb{t}_{cb}")
            s2 = w_acquire(f"gr{t}_{cb}")
            w1 = wview(s1, KC, 256)
            w2 = wview(s2, KC, 256)
            for blk in range(2):
                gblk = cb * 2 + blk
                bY, bG = bank(), bank()
                for kc in range(KC):
                    MM(ps[bY], w1[:, kc, blk * 128:(blk + 1) * 128], ogT[:, kc, :],
                       kc == 0, kc == KC - 1, [("w", s1)] + OGK_ALL, bY)
                for kc in range(KC):
                    MM(ps[bG], w2[:, kc, blk * 128:(blk + 1) * 128], uT[:, kc, :],
                       kc == 0, kc == KC - 1, [("w", s2)] + UT_ALL, bG)
                ACT(sgB, ps[bG], AF.Sigmoid, [], [("ps", bG), ("sgB",)])
                TT_(tprod, ps[bY], sgB, ALU.mult, [("sgB",)], [("ps", bY), ("tprod",)])
                TT_(mergedT[:, gblk, :], tprod, ypt[:, yb + blk, :], ALU.add, [("tprod",), ("XN", yb + blk)],
                    [("XM", gblk // 8)])
                yield
            w_release(s1)
            w_release(s2)

        yield from steps12(0)
        for cb in range(8):
            if cb + 1 < 8:
                yield from steps12(cb + 1)
            if cb == 0:
                drain(bg_holder[0])
                bg_holder[0] = None
            yield from step3(cb)

    def stage_C(xsrc, t):
        for sub in range(NSUB):
            SPDMA(hbuf[:, sub, :], xsrc[t * TT + sub * 128: t * TT + (sub + 1) * 128, :], f"h{sub}",
                  [], HK(sub))
        for cb in range(8):
            s = w_acquire(f"wo{t}_{cb}")
            w = wview(s, KC, 256)
            bks = [bank() for _ in range(NSUB)]
            for kc in range(KC):
                for sub in range(NSUB):
                    MM(ps[bks[sub]][:, 0:256], mergedT[:, kc, sub * 128:(sub + 1) * 128], w[:, kc, :],
                       kc == 0, kc == KC - 1, [("w", s), ("XM", kc // 8)], bks[sub])
            w_release(s)
            for sub in range(NSUB):
                hs = hbuf[:, sub, cb * 256:(cb + 1) * 256]
                TT_(hs, ps[bks[sub]][:, 0:256], hs, ALU.add, HK(sub), [("ps", bks[sub])] + HK(sub))

    def stage_D(t):
        drain(rms_to_T(lambda sub: hbuf[:, sub, :], lambda sub: HK(sub), g2c, D))

    def stage_EF(t, bgA=None):
        sgs = ((sgA, ("sgA",)), (sgB, ("sgB",)))
        for half in range(2):
            for jj in range(11):
                s = w_acquire(f"fia{t}_{half}_{jj}")
                w = wview(s, KC, 256)
                for jsub in range(2):
                    bA = bank()
                    for kc in range(KC):
                        MM(ps[bA], w[:, kc, jsub * 128:(jsub + 1) * 128], uT[:, kc, :],
                           kc == 0, kc == KC - 1, [("w", s)] + UT_ALL, bA)
                    ACT(sgs[jsub][0], ps[bA], AF.Silu, [], [("ps", bA), sgs[jsub][1]])
                w_release(s)
                s = w_acquire(f"fib{t}_{half}_{jj}")
                w = wview(s, KC, 256)
                for jsub in range(2):
                    jl = jj * 2 + jsub
                    bB = bank()
                    for kc in range(KC):
                        MM(ps[bB], w[:, kc, jsub * 128:(jsub + 1) * 128], uT[:, kc, :],
                           kc == 0, kc == KC - 1, [("w", s)] + UT_ALL, bB)
                    TT_(hffT[:, jl, :], sgs[jsub][0], ps[bB], ALU.mult, [sgs[jsub][1]],
                        [("ps", bB)] + HFK(jl))
                w_release(s)
            for cb in range(8):
                bks = [bank() for _ in range(NSUB)]
                for part, nk in (("a", 16), ("b", 6)):
                    s = w_acquire(f"fo{part}{t}_{half}_{cb}")
                    w = wview(s, nk, 256)
                    for kk in range(nk):
                        jl = kk if part == "a" else 16 + kk
                        for sub in range(NSUB):
                            MM(ps[bks[sub]][:, 0:256], hffT[:, jl, sub * 128:(sub + 1) * 128], w[:, kk, :],
                               jl == 0, jl == 21, [("w", s)] + HFK(jl), bks[sub])
                    w_release(s)
                for sub in range(NSUB):
                    hs = hbuf[:, sub, cb * 256:(cb + 1) * 256]
                    TT_(hs, ps[bks[sub]][:, 0:256], hs, ALU.add, HK(sub), [("ps", bks[sub])] + HK(sub))
                if half == 1 and bgA is not None and cb % 2 == 1:
                    next(bgA, None)
        drain(bgA)

    def stage_out(t):
        for sub in range(NSUB):
            src = hbuf[:, sub, :]
            ss, kss = scol()
            rs, krs = scol()
            ri, kri = scol()
            ACT(junk, src, AF.Square, HK(sub), [("junk",), kss], accum=ss)
            ACT(rs, ss, AF.Sqrt, [kss], [krs], scale=1.0 / D, bias=EPS)
            RECIP(ri, rs, [krs], [kri])
            STT(src, src, ri, gfbc, ALU.mult, ALU.mult, HK(sub) + [kri, ("gfbc",)], HK(sub))
            SPDMA(out_d[t * TT + sub * 128: t * TT + (sub + 1) * 128, :], src, f"o{sub}", HK(sub), [])

    nxt = stage_A(xp, 0, ub=0)
    for t in range(NT):
        drain(nxt)
        load_rot(c_rotp, t)
        nxt = stage_A(xp, t + 1, ub=(t + 1) % 2) if t + 1 < NT else stage_A(xm, 0, ub=0)
        bg = None
        for h in range(H):
            merge_streams(gen_proj_prefix(t, h), bg)
            bg = gen_ret(t, h, True, last_prefix_tile=(t == NT - 1))
            if h % 2 == 1:
                next(nxt, None)
        drain(bg)
        if t == NT - 1:
            pz_proj(t, [f"ppz{c4}" for c4 in range(4)], lambda sub: pzhalo, [("pzhalo",)], [NSUB - 1], ub=t % 2)
    nextA = nxt
    for t in range(NT):
        drain(nextA)
        load_rot(c_rotm, t)
        pool_chain(t)
        bg = None
        for h in range(H):
            merge_streams(gen_proj(t, h), bg)
            bg = gen_ret(t, h, False)
        holder = [bg]
        for _ in gen_merge(t, holder):
            if holder[0] is not None:
                if next(holder[0], "end") == "end":
                    holder[0] = None
        drain(holder[0])
        stage_C(xm, t)
        stage_D(t)
        nextA = stage_A(xm, t + 1) if t + 1 < NT else None
        stage_EF(t, nextA)
        nextA = None
        stage_out(t)
    assert wstate["next_acq"] == len(units), (wstate, len(units))

    P.finalize()
    sem_names = list(ENGS[:4]) + sorted(P.dma_cnt.keys())
    sems = {n: st.enter_context(nc.semaphore(n)) for n in sem_names}
    block = st.enter_context(nc.Block())

    @block.tensor
    def _(e):
        P.emit_engine("pe", e, sems, lookahead=40)

    @block.scalar
    def _(e):
        P.emit_engine("act", e, sems)

    @block.vector
    def _(e):
        P.emit_engine("dve", e, sems)

    @block.gpsimd
    def _(e):
        P.emit_engine("pool", e, sems)

    @block.sync
    def _(e):
        P.emit_engine("sp", e, sems)
        for n in ("o0", "o1", "o2", "o3"):
            e.wait_ge(sems[n], P.dma_cnt[n])

    st.close()
    return nc


def _const_tables(pos0_main, pos0_pre, first_is_seq_start):
    h = np.arange(H, dtype=np.float64)
    gam = 1.0 - 2.0 ** (-5.0 - h)
    n = np.arange(128, dtype=np.float64)
    m = n[:, None]
    nn = n[None, :]
    same = (m // 64) == (nn // 64)
    low = (m // 64 == 0) & (nn // 64 == 1)
    mask = np.zeros((H, 128, 128), np.float64)
    for i in range(H):
        mk = np.where(same, gam[i] ** np.abs(nn - m), 0.0) + np.where(low, gam[i] ** (nn - m), 0.0)
        mask[i] = mk * gam[i] ** (-(nn + 1.0))
    c_mask = np.ascontiguousarray(mask.transpose(1, 0, 2).reshape(128, H * 128)).astype(np.float32)
    kdec = (gam[None, :] ** (127.0 - n[:, None])).astype(np.float32)
    epsh = (EPS * gam[None, :] ** (-2.0 * (n[:, None] + 1.0))).astype(np.float32)

    def rot(pos0):
        inv_freq = (1.0 / (10000.0 ** (np.arange(0, 256, 2, dtype=np.float32) / np.float32(256)))).astype(np.float32)
        pos = (pos0 + np.arange(NTOK)).astype(np.float32)
        ang = (pos[None, :] * inv_freq[:, None]).astype(np.float32)
        r = np.stack([np.cos(ang), np.sin(ang)], axis=1).astype(np.float32)
        return np.ascontiguousarray(r * np.float32(1.0 / 16.0))

    ag = np.zeros((128, 12, 128), np.float32)
    s_ = np.arange(128)[:, None]
    t_ = np.arange(128)[None, :]
    for gi, w in enumerate((2, 4, 8, 16)):
        gen = ((s_ <= t_) & (s_ > t_ - w)).astype(np.float64) / w - (s_ == t_)
        halo = ((s_ - 128) > (t_ - w)).astype(np.float64) / w
        cnt = np.minimum(t_ + 1, w).astype(np.float64)
        first = ((s_ <= t_) & (s_ > t_ - w)).astype(np.float64) / cnt - (s_ == t_)
        ag[:, gi, :] = gen
        ag[:, 4 + gi, :] = halo
        ag[:, 8 + gi, :] = first if first_is_seq_start else gen
    return c_mask, kdec, epsh, rot(pos0_main), rot(pos0_pre), np.ascontiguousarray(ag.reshape(128, 12 * 128))


_NC_CACHE = {}


def kernel(x, norm1_g, w_in, w_ret_branch, w_pool_group, pool_scale, w_pool_branch,
           w_out, norm2_g, w_ffn_in, w_ffn_out, norm_final_g):
    f32 = np.float32
    x = np.asarray(x, f32)
    B, S, _ = x.shape
    ncores = 8
    if "nc" not in _NC_CACHE:
        _NC_CACHE["nc"] = build_nc()
    nc = _NC_CACHE["nc"]

    def colmajor(v, nblk):
        return np.ascontiguousarray(np.asarray(v, f32).reshape(nblk, 128).T)

    shared = {
        "w_in": np.ascontiguousarray(np.asarray(w_in, f32)[0]),
        "w_rb": np.ascontiguousarray(np.asarray(w_ret_branch, f32)[0]),
        "w_pg": np.ascontiguousarray(np.asarray(w_pool_group, f32)[0].reshape(1024, 256)),
        "w_pb": np.ascontiguousarray(np.asarray(w_pool_branch, f32)[0]),
        "w_o": np.ascontiguousarray(np.asarray(w_out, f32)[0]),
        "w_fi": np.ascontiguousarray(np.asarray(w_ffn_in, f32)[0]),
        "w_fo": np.ascontiguousarray(np.asarray(w_ffn_out, f32)[0]),
        "c_gf": np.ascontiguousarray(np.broadcast_to(np.asarray(norm_final_g, f32)[None, :], (128, D))),
        "c_ident": np.eye(128, dtype=f32),
    }
    g1c = colmajor(norm1_g[0], 16)
    g2c = colmajor(norm2_g[0], 16)
    psc = colmajor(pool_scale[0], 8)
    in_maps = []
    zeros = np.zeros((NTOK, D), f32)
    for c in range(ncores):
        b, half = c // 2, c % 2
        c_mask, kdec, epsh, rotm, rotp, ag = _const_tables(half * NTOK, 0, half == 0)
        small = np.zeros((128, 64), f32)
        small[:, 0:16] = g1c
        small[:, 16:32] = g2c
        small[:, 32:40] = psc
        small[:, 40:48] = kdec
        small[:, 48:56] = epsh
        m = dict(shared)
        m["xm"] = np.ascontiguousarray(x[b, half * NTOK:(half + 1) * NTOK])
        m["xp"] = np.ascontiguousarray(x[b, 0:NTOK]) if half == 1 else zeros
        m["c_small"] = small
        m["c_mask"] = c_mask
        m["c_ag"] = ag
        m["c_rotm"] = rotm
        m["c_rotp"] = rotp
        in_maps.append(m)
    res = run_bass_kernel_spmd(nc, in_maps, core_ids=list(range(ncores)))
    out = np.empty((B, S, D), f32)
    for c in range(ncores):
        b, half = c // 2, c % 2
        out[b, half * NTOK:(half + 1) * NTOK] = res.results[c]["out"]
    return out
```

```python
import numpy as np
from contextlib import ExitStack
import concourse.bass as bass
import concourse.mybir as mybir
from concourse.bass_utils import run_bass_kernel_spmd

F32 = mybir.dt.float32
BF16 = mybir.dt.bfloat16
ALU = mybir.AluOpType
AF = mybir.ActivationFunctionType

D = 2048
KC = 16
TT = 512
NSUB = 4
NTOK = 2048
NT = NTOK // TT
H = 8
DFF = 5632
NJ = DFF // 128
EPS = 1e-6
C_Q, C_K, C_V, C_RG, C_PZ, C_GR, C_GP = 0, 2048, 4096, 6144, 8192, 9216, 11264
NSLOT = 6
ENGS = ("pe", "act", "dve", "pool", "sp")


class Op:
    __slots__ = ("eng", "fn", "deps", "sig", "cnt", "dma_sem", "dma_val", "idx")

    def __init__(self, eng, fn, dma_sem=None):
        self.eng, self.fn, self.dma_sem = eng, fn, dma_sem
        self.deps, self.sig, self.cnt, self.dma_val, self.idx = [], False, 0, 0, 0


class Prog:
    def __init__(self):
        self.ops = {e: [] for e in ENGS}
        self.wr, self.rd = {}, {}
        self.dma_cnt = {}
        self.n = 0

    def _add(self, op, reads, writes):
        deps = {}

        def need(x, raw):
            if x is None:
                return
            if x.dma_sem is None:
                if x.eng == op.eng and op.dma_sem is None:
                    if op.eng == "pe":
                        return
                k = x.eng
            else:
                k = ("dma", x.dma_sem)
            if k not in deps or deps[k].idx < x.idx:
                deps[k] = x

        for r in reads:
            need(self.wr.get(r), True)
        for w in writes:
            need(self.wr.get(w), False)
            for x in self.rd.get(w, {}).values():
                need(x, False)
        op.deps = []
        for x in deps.values():
            if x.dma_sem is None:
                x.sig = True
                op.deps.append((x, None, None))
            else:
                op.deps.append((None, x.dma_sem, self.dma_cnt[x.dma_sem]))
        me = op.eng if op.dma_sem is None else ("dma", op.dma_sem)
        for r in reads:
            self.rd.setdefault(r, {})[me] = op
        for w in writes:
            self.wr[w] = op
            self.rd[w] = {}
        op.idx = self.n
        self.n += 1
        self.ops[op.eng].append(op)
        return op

    def op(self, eng, fn, reads=(), writes=()):
        return self._add(Op(eng, fn), list(reads), list(writes))

    def dma(self, eng, fn, sem, reads=(), writes=()):
        o = Op(eng, fn, dma_sem=sem)
        self.dma_cnt.setdefault(sem, 0)
        self._add(o, list(reads), list(writes))
        self.dma_cnt[sem] += 16
        o.dma_val = self.dma_cnt[sem]
        return o

    def emit_engine(self, e, handle, sems, lookahead=0):
        known = {}
        ops = self.ops[e]
        for i, op in enumerate(ops):
            for (x, dsem, dval) in op.deps:
                if x is not None:
                    name, val = x.eng, x.cnt
                else:
                    name, val = dsem, dval
                if known.get(name, 0) >= val:
                    continue
                if x is not None and lookahead:
                    for op2 in ops[i + 1:i + 1 + lookahead]:
                        for (x2, _, _) in op2.deps:
                            if x2 is not None and x2.eng == name and x2.idx < op.idx and x2.cnt > val:
                                val = x2.cnt
                handle.wait_ge(sems[name], val)
                known[name] = val
            inst = op.fn(handle)
            if op.dma_sem is not None:
                inst.then_inc(sems[op.dma_sem], 16)
            elif op.sig:
                inst.then_inc(sems[e], 1)

    def finalize(self):
        for e in ENGS:
            c = 0
            for op in self.ops[e]:
                if op.dma_sem is None and op.sig:
                    c += 1
                    op.cnt = c


def merge_streams(main, bg):
    acc = 0
    while True:
        try:
            w = next(main)
        except StopIteration:
            break
        acc += 2 if w is None else w
        while acc >= 2 and bg is not None:
            acc -= 2
            try:
                next(bg)
            except StopIteration:
                bg = None
    if bg is not None:
        for _ in bg:
            pass


def drain(g):
    if g is not None:
        for _ in g:
            pass


def build_nc(debug_taps=False):
    nc = bass.Bass("TRN2", target_bir_lowering=False)
    P = Prog()

    def din(name, shape):
        return nc.dram_tensor(name, list(shape), F32, kind="ExternalInput").ap()

    xm = din("xm", [NTOK, D])
    xp = din("xp", [NTOK, D])
    w_in = din("w_in", [D, 13312]).rearrange("(kc p) c -> p kc c", p=128)
    w_rb = din("w_rb", [D, D]).rearrange("(kc p) c -> p kc c", p=128)
    w_pg = din("w_pg", [1024, 256]).rearrange("(kc p) c -> p kc c", p=128)
    w_pb = din("w_pb", [1024, D]).rearrange("(kc p) c -> p kc c", p=128)
    w_o = din("w_o", [D, D]).rearrange("(kc p) c -> p kc c", p=128)
    w_fi = din("w_fi", [D, 2 * DFF]).rearrange("(kc p) c -> p kc c", p=128)
    w_fo = din("w_fo", [DFF, D]).rearrange("(kc p) c -> p kc c", p=128)
    c_small = din("c_small", [128, 64])
    c_gf = din("c_gf", [128, D])
    c_mask = din("c_mask", [128, H * 128])
    c_ident = din("c_ident", [128, 128])
    c_ag = din("c_ag", [128, 12 * 128])
    c_rotm = din("c_rotm", [128, 2, NTOK])
    c_rotp = din("c_rotp", [128, 2, NTOK])
    out_d = nc.dram_tensor("out", [NTOK, D], F32, kind="ExternalOutput").ap()

    st = ExitStack()

    def sb(name, shape, dt):
        return st.enter_context(nc.sbuf_tensor(name, list(shape), dt))[:]

    ident32 = sb("ident32", [128, 128], F32)
    identbf = sb("identbf", [128, 128], BF16)
    csm = sb("csm", [128, 64], F32)
    maskT = sb("maskT", [128, H, 128], F32)
    AgT = sb("AgT", [128, 12, 128], BF16)
    rot = sb("rot", [128, 2, TT], F32)
    XM = sb("XM", [128, 2 * D], F32)
    XN = sb("XN", [128, D], F32)
    junk = sb("junk", [128, D], BF16)
    uT = sb("uT", [128, KC, TT], BF16)
    st32 = sb("st32", [128, H, 512], F32)
    stbf = sb("stbf", [128, H, 512], BF16)
    rtmp1 = sb("rtmp1", [128, TT], F32)
    rtmp2 = sb("rtmp2", [128, TT], F32)
    sgA = sb("sgA", [128, TT], F32)
    sgB = sb("sgB", [128, TT], F32)
    tprod = sb("tprod", [128, TT], F32)
    R2 = sb("R2", [128, 12288], BF16)
    R3 = sb("R3", [128, 8192], F32)
    pzhalo = sb("pzhalo", [128, 1024], BF16)
    gfbc = sb("gfbc", [128, D], F32)
    small = sb("small", [128, 64], F32)
    ring = [sb(f"ring{i}", [128, 4096], BF16) for i in range(NSLOT)]
    ps = [st.enter_context(nc.psum_tensor(f"ps{i}", [128, 512], F32))[:] for i in range(8)]

    g1c = csm[:, 0:16]
    g2c = csm[:, 16:32]
    psc = csm[:, 32:40]
    kdec = csm[:, 40:48]
    epsh = csm[:, 48:56]

    xin = [XM[:, 0:D], XM[:, D:2 * D]]
    mergedT = XM.bitcast(BF16).rearrange("p (a b) -> p a b", a=KC)
    ypt = XN.rearrange("p (a b) -> p a b", a=4)
    ogT = R2[:, 0:8192].rearrange("p (a b) -> p a b", a=KC)
    pooled2T = R2[:, 8192:12288].rearrange("p (a b) -> p a b", a=8)
    hffT = R2[:, 0:11264].rearrange("p (a b) -> p a b", a=22)
    hbuf = R3.rearrange("p (a b) -> p a b", a=NSUB)
    R3b = R3.bitcast(BF16)

    def r3keys(off, nbytes):
        return [("R3", i) for i in range(off // 2048, (off + nbytes + 2047) // 2048)]

    def r3_bf(off, shape):
        n = int(np.prod(shape))
        ap = R3b[:, off // 2: off // 2 + n]
        if len(shape) == 2:
            ap = ap.rearrange("p (a b) -> p a b", a=shape[0])
        return ap, r3keys(off, n * 2)

    def r3_f32(off, shape):
        n = int(np.prod(shape))
        ap = R3[:, off // 4: off // 4 + n]
        if len(shape) == 2:
            ap = ap.rearrange("p (a b) -> p a b", a=shape[0])
        return ap, r3keys(off, n * 4)

    K = 1024
    qT = [r3_bf(0, [2, TT]), r3_bf(2 * K, [2, TT])]
    kT = [r3_bf(4 * K, [2, TT]), r3_bf(6 * K, [2, TT])]
    kd = [r3_bf(8 * K, [NSUB, 256]), r3_bf(10 * K, [NSUB, 256])]
    vv = [r3_bf(12 * K, [NSUB, 256]), r3_bf(14 * K, [NSUB, 256])]
    og = [r3_bf(16 * K, [NSUB, 256]), r3_bf(18 * K, [NSUB, 256])]
    srg = [r3_f32(20 * K, [NSUB, 256]), r3_f32(24 * K, [NSUB, 256])]
    sT = [r3_bf(28 * K, [128]), r3_bf(30 * K, [128])]
    def r2keys(off, nbytes):
        return [("R2", i) for i in range(off // 2048, (off + nbytes + 2047) // 2048)]
    pz_tm = (R2[:, 0:4096].rearrange("p (a b) -> p a b", a=NSUB), r2keys(0, 8192))
    pooledT = (R2[:, 4096:8192].rearrange("p (a b) -> p a b", a=8), r2keys(8192, 8192))
    OGK = lambda h: [("R2", h)]
    OGK_ALL = [("R2", i) for i in range(8)]
    P2K = r2keys(16384, 8192)
    HFK = lambda jl: [("R2", jl // 2)]
    HK = lambda sub: r3keys(sub * 8192, 8192)

    _sc = [0]

    def scol(n=1):
        c = _sc[0]
        _sc[0] = (c + 1) % 64
        return small[:, c:c + 1], ("small", c)

    _bk = [0]

    _bq = [0]
    _bg = [0]

    def bank(cls=None):
        if cls == "qk":
            b = _bq[0]
            _bq[0] = (b + 1) % 4
            return b
        if cls == "gen":
            b = 4 + _bg[0]
            _bg[0] = (_bg[0] + 1) % 4
            return b
        b = _bk[0]
        _bk[0] = (b + 1) % 8
        return b

    deferred = []

    def flush_deferred(n=None):
        k = len(deferred) if n is None else min(n, len(deferred))
        for _ in range(k):
            deferred.pop(0)()

    units = []
    wstate = {"next_issue": 0, "next_acq": 0, "released": 0}

    def slot3(i, a, b):
        return ring[i].rearrange("p (a b) -> p a b", a=a)

    def w_issue(u):
        name, parts = units[u]
        s = u % NSLOT
        for (dst_fn, src) in parts:
            dst = dst_fn(s)
            P.dma("pool", (lambda d, s_: (lambda e: e.dma_start(out=d, in_=s_)))(dst, src),
                  sem=f"w{s}", writes=[("w", s)])

    def w_pump():
        while wstate["next_issue"] < len(units) and wstate["next_issue"] - NSLOT < wstate["released"]:
            w_issue(wstate["next_issue"])
            wstate["next_issue"] += 1

    def w_acquire(name):
        u = wstate["next_acq"]
        assert units[u][0] == name, (units[u][0], name)
        wstate["next_acq"] += 1
        w_pump()
        assert u < wstate["next_issue"]
        return u % NSLOT

    def w_release(s):
        assert wstate["released"] % NSLOT == s
        wstate["released"] += 1
        w_pump()

    def col_unit(name, w, c0, ncols, nk=KC, k0=0):
        parts = [((lambda s, nk=nk, ncols=ncols:
                   ring[s][:, 0:nk * ncols].rearrange("p (a b) -> p a b", a=nk)),
                  w[:, k0:k0 + nk, c0:c0 + ncols])]
        units.append((name, parts))

    def col_unit2(name, w, ca, cb_, nc_each):
        def dst(off):
            return lambda s: ring[s][:, 0:KC * 2 * nc_each].rearrange(
                "p (a b) -> p a b", a=KC)[:, :, off:off + nc_each]
        units.append((name, [(dst(0), w[:, :, ca:ca + nc_each]),
                             (dst(nc_each), w[:, :, cb_:cb_ + nc_each])]))

    def wview(s, nk, ncols):
        return ring[s][:, 0:nk * ncols].rearrange("p (a b) -> p a b", a=nk)

    for t in range(NT):
        for h in range(H):
            col_unit(f"pk{t}_{h}", w_in, C_K + h * 256, 256)
            col_unit(f"pv{t}_{h}", w_in, C_V + h * 256, 256)
        if t == NT - 1:
            for c4 in range(4):
                col_unit(f"ppz{c4}", w_in, C_PZ + c4 * 256, 256)
    for t in range(NT):
        for c4 in range(4):
            col_unit(f"pz{c4}_{t}", w_in, C_PZ + c4 * 256, 256)
        col_unit(f"wpg_{t}", w_pg, 0, 256, nk=8)
        for h in range(H):
            col_unit(f"q{t}_{h}", w_in, C_Q + h * 256, 256)
            col_unit(f"k{t}_{h}", w_in, C_K + h * 256, 256)
            col_unit(f"v{t}_{h}", w_in, C_V + h * 256, 256)
            col_unit(f"rg{t}_{h}", w_in, C_RG + h * 256, 256)
        def _u12(cb):
            col_unit(f"wpb{t}_{cb}", w_pb, cb * 256, 256, nk=8)
            col_unit(f"gp{t}_{cb}", w_in, C_GP + cb * 256, 256)
        _u12(0)
        for cb in range(8):
            if cb + 1 < 8:
                _u12(cb + 1)
            col_unit(f"wrb{t}_{cb}", w_rb, cb * 256, 256)
            col_unit(f"gr{t}_{cb}", w_in, C_GR + cb * 256, 256)
        for cb in range(8):
            col_unit(f"wo{t}_{cb}", w_o, cb * 256, 256)
        for half in range(2):
            for jj in range(11):
                j0 = half * 22 + jj * 2
                col_unit(f"fia{t}_{half}_{jj}", w_fi, j0 * 128, 256)
                col_unit(f"fib{t}_{half}_{jj}", w_fi, DFF + j0 * 128, 256)
            for cb in range(8):
                col_unit(f"foa{t}_{half}_{cb}", w_fo, cb * 256, 256, nk=16, k0=half * 22)
                col_unit(f"fob{t}_{half}_{cb}", w_fo, cb * 256, 256, nk=6, k0=half * 22 + 16)

    def MM(out, lhsT, rhs, start, stop, reads, bk):
        P.op("pe", lambda e: e.matmul(out, lhsT, rhs, start=start, stop=stop),
             reads=reads, writes=[("ps", bk)])

    def TR(out, in_, ident, reads, bk):
        P.op("pe", lambda e: e.transpose(out, in_, ident), reads=reads, writes=[("ps", bk)])

    def ACT(out, in_, func, reads, writes, scale=1.0, bias=0.0, accum=None):
        if accum is None:
            P.op("act", lambda e: e.activation(out, in_, func, bias=bias, scale=scale),
                 reads=reads, writes=writes)
        else:
            P.op("act", lambda e: e.activation(out, in_, func, bias=bias, scale=scale,
                                               accum_out=accum), reads=reads, writes=writes)

    def TT_(out, a, b, op, reads, writes, eng="dve"):
        P.op(eng, lambda e: e.tensor_tensor(out, a, b, op), reads=reads, writes=writes)

    def STT(out, in0, scalar, in1, op0, op1, reads, writes):
        P.op("dve", lambda e: e.scalar_tensor_tensor(out, in0, scalar, in1, op0, op1),
             reads=reads, writes=writes)

    def TS(out, in0, s1, op0, reads, writes, s2=None, op1=None):
        if op1 is None:
            P.op("dve", lambda e: e.tensor_scalar(out, in0, s1, None, op0), reads=reads, writes=writes)
        else:
            P.op("dve", lambda e: e.tensor_scalar(out, in0, s1, s2, op0, op1), reads=reads, writes=writes)

    def COPYV(out, in_, reads, writes):
        P.op("dve", lambda e: e.tensor_copy(out, in_), reads=reads, writes=writes)

    def RECIP(out, in_, reads, writes):
        P.op("dve", lambda e: e.reciprocal(out, in_), reads=reads, writes=writes)

    def SPDMA(out, in_, sem, reads, writes):
        P.dma("sp", lambda e: e.dma_start(out=out, in_=in_), sem=sem, reads=reads, writes=writes)

    SPDMA(ident32, c_ident, "c0", [], [("ident32",)])
    SPDMA(csm, c_small, "c0", [], [("csm",)])
    SPDMA(maskT.rearrange("p a b -> p (a b)"), c_mask, "c0", [], [("maskT",)])
    SPDMA(gfbc, c_gf, "c0", [], [("gfbc",)])
    P.dma("pool", lambda e: e.dma_start(out=identbf, in_=c_ident), sem="c1", writes=[("identbf",)])
    P.dma("pool", lambda e: e.dma_start(out=AgT.rearrange("p a b -> p (a b)"), in_=c_ag),
          sem="c1", writes=[("AgT",)])
    P.op("dve", lambda e: e.memset(st32.rearrange("p a b -> p (a b)"), 0.0),
         writes=[("st32", h) for h in range(H)])
    P.op("dve", lambda e: e.memset(stbf.rearrange("p a b -> p (a b)"), 0.0),
         writes=[("stbf", h) for h in range(H)])

    xcnt = [0]

    uT_alt = R2[:, 0:8192].rearrange("p (a b) -> p a b", a=KC)
    UTB = [(uT, lambda sub: [("uT", sub)], [("uT", s_) for s_ in range(NSUB)]),
           (uT_alt, lambda sub: [("R2", i) for i in range(8)], [("R2", i) for i in range(8)])]

    def rms_to_T(src_ap_fn, src_reads_fn, gcols, n_in, load_fn=None, ub=0, inplace=False, dst_bufs=None):
        uTd, ukey_fn, _ = UTB[ub]

        def norm(sub):
            if load_fn is not None:
                load_fn(sub)
            src = src_ap_fn(sub)
            sreads = src_reads_fn(sub)
            ss, kss = scol()
            rs, krs = scol()
            ri, kri = scol()
            ACT(junk, src, AF.Square, sreads, [("junk",), kss], accum=ss)
            ACT(rs, ss, AF.Sqrt, [kss], [krs], scale=1.0 / n_in, bias=EPS)
            RECIP(ri, rs, [krs], [kri])
            if inplace:
                TS(src, src, ri, ALU.mult, sreads + [kri], sreads)
                return src, sreads
            if dst_bufs is not None:
                d, dk = dst_bufs[sub % 2]
                TS(d, src, ri, ALU.mult, sreads + [kri], dk)
                return d, dk
            TS(XN, src, ri, ALU.mult, sreads + [kri], XNK)
            return XN, XNK

        def trans(sub, xn, xkeys):
            for q in range(4):
                bk = bank()
                for i in range(4):
                    kc = q * 4 + i
                    TR(ps[bk][:, i * 128:(i + 1) * 128], xn[:, kc * 128:(kc + 1) * 128], ident32,
                       xkeys + [("ident32",)], bk)
                gb = gcols[:, q * 4:q * 4 + 4].unsqueeze(2).to_broadcast([128, 4, 128])
                TT_(uTd[:, q * 4:q * 4 + 4, sub * 128:(sub + 1) * 128],
                    ps[bk].rearrange("p (a b) -> p a b", a=4), gb, ALU.mult,
                    [("csm",)], [("ps", bk)] + ukey_fn(sub))

        if not inplace and dst_bufs is None:
            for sub in range(NSUB):
                xn, xk = norm(sub)
                trans(sub, xn, xk)
                yield
            return
        cur = norm(0)
        yield
        for sub in range(NSUB):
            nxt_ = norm(sub + 1) if sub + 1 < NSUB else None
            trans(sub, *cur)
            cur = nxt_
            yield

    def stage_A(xsrc, t, ub=0):
        def load(sub):
            i = xcnt[0] % 2
            xcnt[0] += 1
            load.cur = i
            SPDMA(xin[i], xsrc[t * TT + sub * 128: t * TT + (sub + 1) * 128, :], f"x{i}", [], [("XM", i)])
        return rms_to_T(lambda sub: xin[load.cur], lambda sub: [("XM", load.cur)], g1c, D, load_fn=load,
                        ub=ub, inplace=True)

    UT_ALL = [("uT", s) for s in range(NSUB)]
    XNK = [("XN", i) for i in range(4)]

    def load_rot(src, t):
        SPDMA(rot, src[:, :, t * TT:(t + 1) * TT], "rot", [], [("rot",)])

    def rotary_defer(bA, bB, dst, dkeys):
        cos, sin = rot[:, 0, :], rot[:, 1, :]
        deferred.append(lambda: TT_(rtmp1, ps[bA], cos, ALU.mult, [("rot",)], [("ps", bA), ("rt1",)]))
        deferred.append(lambda: TT_(rtmp2, ps[bB], sin, ALU.mult, [("rot",)], [("ps", bB), ("rt2",)]))
        deferred.append(lambda: TT_(dst[:, 0, :], rtmp1, rtmp2, ALU.subtract, [("rt1",), ("rt2",)], dkeys))
        deferred.append(lambda: TT_(rtmp1, ps[bA], sin, ALU.mult, [("rot",)], [("ps", bA), ("rt1",)]))
        deferred.append(lambda: TT_(rtmp2, ps[bB], cos, ALU.mult, [("rot",)], [("ps", bB), ("rt2",)]))
        deferred.append(lambda: TT_(dst[:, 1, :], rtmp1, rtmp2, ALU.add, [("rt1",), ("rt2",)], dkeys))

    def proj_rot_steps(s, w, coff, dst, dkeys, ub=0):
        uTs, ukey_fn, ukeys_all = UTB[ub]
        bA, bB = bank("qk"), bank("qk")
        for dblk, bk in ((0, bA), (1, bB)):
            for kc in range(KC):
                MM(ps[bk], w[:, kc, coff + dblk * 128: coff + (dblk + 1) * 128], uTs[:, kc, :],
                   kc == 0, kc == KC - 1, [("w", s)] + ukeys_all, bk)
            if dblk == 1:
                rotary_defer(bA, bB, dst, dkeys)
            flush_deferred(2)
            yield

    def v_like_steps(name, dst, dkeys, func, nflush, ub=0):
        uTs, ukey_fn, ukeys_all = UTB[ub]
        s = w_acquire(name)
        w = wview(s, KC, 256)
        for sub in range(NSUB):
            bk = bank("gen")
            for kc in range(KC):
                MM(ps[bk][:, 0:256], uTs[:, kc, sub * 128:(sub + 1) * 128], w[:, kc, :],
                   kc == 0, kc == KC - 1, [("w", s)] + ukey_fn(sub), bk)
            ACT(dst[:, sub, :], ps[bk][:, 0:256], func, [], [("ps", bk)] + dkeys)
            flush_deferred(nflush)
            yield 1
        w_release(s)

    def gen_proj_prefix(t, h):
        hp = h % 2
        s = w_acquire(f"pk{t}_{h}")
        yield from proj_rot_steps(s, wview(s, KC, 256), 0, *kT[hp], ub=t % 2)
        w_release(s)
        yield from v_like_steps(f"pv{t}_{h}", vv[hp][0], vv[hp][1], AF.Copy, 2, ub=t % 2)
        flush_deferred()

    def gen_proj(t, h):
        hp = h % 2
        s = w_acquire(f"q{t}_{h}")
        yield from proj_rot_steps(s, wview(s, KC, 256), 0, *qT[hp])
        w_release(s)
        s = w_acquire(f"k{t}_{h}")
        yield from proj_rot_steps(s, wview(s, KC, 256), 0, *kT[hp])
        w_release(s)
        yield from v_like_steps(f"v{t}_{h}", vv[hp][0], vv[hp][1], AF.Copy, 1)
        yield from v_like_steps(f"rg{t}_{h}", srg[hp][0], srg[hp][1], AF.Silu, 1)
        flush_deferred()

    def gen_ret(t, h, prefix, last_prefix_tile=False):
        hp = h % 2
        g = 1.0 - 2.0 ** (-5.0 - h)
        g128 = float(np.float64(g) ** 128)
        kTa, kTk = kT[hp]
        qTa, qTk = qT[hp]
        kda, kdk = kd[hp]
        va, vk = vv[hp]
        oga, ogk = og[hp]
        sa, sk = srg[hp]
        bk = bank("gen")
        bv = ps[bk].bitcast(BF16).rearrange("p (a b) -> p a b", a=NSUB)
        for sub in range(NSUB):
            for dblk in range(2):
                TR(bv[:, sub, dblk * 128:(dblk + 1) * 128], kTa[:, dblk, sub * 128:(sub + 1) * 128],
                   identbf, kTk + [("identbf",)], bk)
        ACT(kda.rearrange("p a b -> p (a b)"), ps[bk].bitcast(BF16), AF.Copy, [("csm",)],
            [("ps", bk)] + kdk, scale=kdec[:, h:h + 1])
        yield

        def phase_A(sub):
            tk = slice(sub * 128, (sub + 1) * 128)
            sTa, sTk = sT[sub % 2]
            bS = bank("gen")
            for dblk in range(2):
                MM(ps[bS][:, 0:128], kTa[:, dblk, tk], qTa[:, dblk, tk], dblk == 0, dblk == 1,
                   kTk + qTk, bS)
            TT_(sTa, ps[bS][:, 0:128], maskT[:, h, :], ALU.mult, [("maskT",)], [("ps", bS)] + sTk)

        def phase_S(sub, copy_bf):
            bSt = bank("gen")
            for dblk in range(2):
                MM(ps[bSt][:, dblk * 256:(dblk + 1) * 256], kda[:, sub, dblk * 128:(dblk + 1) * 128],
                   va[:, sub, :], True, True, kdk + vk, bSt)
            STT(st32[:, h, :], st32[:, h, :], g128, ps[bSt], ALU.mult, ALU.add,
                [("st32", h)], [("ps", bSt), ("st32", h)])
            if copy_bf:
                ACT(stbf[:, h, :], st32[:, h, :], AF.Copy, [("st32", h)], [("stbf", h)])

        def phase_B(sub):
            tk = slice(sub * 128, (sub + 1) * 128)
            sTa, sTk = sT[sub % 2]
            bO = bank("gen")
            MM(ps[bO][:, 0:256], sTa, va[:, sub, :], True, False, sTk + vk, bO)
            for dblk in range(2):
                MM(ps[bO][:, 0:256], qTa[:, dblk, tk], stbf[:, h, dblk * 256:(dblk + 1) * 256],
                   False, dblk == 1, qTk + [("stbf", h)], bO)
            phase_S(sub, True)
            ss, kss = scol()
            rs, krs = scol()
            ri, kri = scol()
            ACT(junk[:, 0:256], ps[bO][:, 0:256], AF.Square, [], [("ps", bO), ("junk",), kss], accum=ss)
            ACT(rs, ss, AF.Sqrt, [kss, ("csm",)], [krs], scale=1.0 / 256, bias=epsh[:, h:h + 1])
            RECIP(ri, rs, [krs], [kri])
            STT(oga[:, sub, :], ps[bO][:, 0:256], ri, sa[:, sub, :], ALU.mult, ALU.mult,
                [kri] + sk, [("ps", bO)] + ogk)

        if prefix:
            for sub in range(NSUB):
                phase_S(sub, last_prefix_tile and sub == NSUB - 1)
                if sub % 2 == 1:
                    yield
            return
        phase_A(0)
        yield
        for sub in range(NSUB):
            phase_B(sub)
            if sub + 1 < NSUB:
                phase_A(sub + 1)
            yield
        bk = bank("gen")
        bv = ps[bk].bitcast(BF16).rearrange("p (a b) -> p a b", a=2)
        for sub in range(NSUB):
            for dvb in range(2):
                TR(bv[:, dvb, sub * 128:(sub + 1) * 128], oga[:, sub, dvb * 128:(dvb + 1) * 128],
                   identbf, ogk + [("identbf",)], bk)
        ACT(ogT[:, 2 * h:2 * h + 2, :].rearrange("p a b -> p (a b)"), ps[bk].bitcast(BF16), AF.Copy,
            [], [("ps", bk)] + OGK(h))
        yield

    def pz_proj(t, names, dst_fn, dkeys, subs, ub=0):
        uTs, ukey_fn, ukeys_all = UTB[ub]
        for c4, name in enumerate(names):
            s = w_acquire(name)
            w = wview(s, KC, 256)
            for sub in subs:
                bk = bank()
                for kc in range(KC):
                    MM(ps[bk][:, 0:256], uTs[:, kc, sub * 128:(sub + 1) * 128], w[:, kc, :],
                       kc == 0, kc == KC - 1, [("w", s)] + ukey_fn(sub), bk)
                ACT(dst_fn(sub)[:, c4 * 256:(c4 + 1) * 256], ps[bk][:, 0:256], AF.Copy, [],
                    [("ps", bk)] + dkeys)
            w_release(s)

    def pool_chain(t):
        pza, pzk = pz_tm
        pla, plk = pooledT
        pz_proj(t, [f"pz{c4}_{t}" for c4 in range(4)], lambda sub: pza[:, sub, :], pzk, range(NSUB))
        for sub in range(NSUB):
            for half in range(2):
                bk = bank()
                for i in range(4):
                    cblk = half * 4 + i
                    gidx = cblk // 2
                    a_main = AgT[:, (8 + gidx) if (t == 0 and sub == 0) else gidx, :]
                    a_halo = AgT[:, 4 + gidx, :]
                    cs = slice(cblk * 128, (cblk + 1) * 128)
                    o = ps[bk][:, i * 128:(i + 1) * 128]
                    MM(o, pza[:, sub, cs], a_main, True, False, pzk + [("AgT",)], bk)
                    if sub == 0:
                        MM(o, pzhalo[:, cs], a_halo, False, True, [("pzhalo",), ("AgT",)], bk)
                    else:
                        MM(o, pza[:, sub - 1, cs], a_halo, False, True, pzk + [("AgT",)], bk)
                COPYV(pla[:, half * 4:half * 4 + 4, sub * 128:(sub + 1) * 128],
                      ps[bk].rearrange("p (a b) -> p a b", a=4), [], [("ps", bk)] + plk)
        COPYV(pzhalo, pza[:, NSUB - 1, :], pzk, [("pzhalo",)])
        s = w_acquire(f"wpg_{t}")
        w = wview(s, 8, 256)
        for dblk in range(8):
            gidx, dh = dblk // 2, dblk % 2
            bk = bank()
            for ch in range(2):
                MM(ps[bk], w[:, gidx * 2 + ch, dh * 128:(dh + 1) * 128], pla[:, gidx * 2 + ch, :],
                   ch == 0, ch == 1, [("w", s)] + plk, bk)
            TS(pooled2T[:, dblk, :], ps[bk], psc[:, dblk:dblk + 1], ALU.mult, [("csm",)],
               [("ps", bk)] + P2K)
        w_release(s)

    def gen_merge(t, bg_holder):
        def steps12(cb):
            yb = (cb % 2) * 2
            s = w_acquire(f"wpb{t}_{cb}")
            w = wview(s, 8, 256)
            for blk in range(2):
                bk = bank()
                for kc in range(8):
                    MM(ps[bk], w[:, kc, blk * 128:(blk + 1) * 128], pooled2T[:, kc, :],
                       kc == 0, kc == 7, [("w", s)] + P2K, bk)
                ACT(ypt[:, yb + blk, :], ps[bk], AF.Copy, [], [("ps", bk), ("XN", yb + blk)])
                yield
            w_release(s)
            s = w_acquire(f"gp{t}_{cb}")
            w = wview(s, KC, 256)
            for blk in range(2):
                bk = bank()
                for kc in range(KC):
                    MM(ps[bk], w[:, kc, blk * 128:(blk + 1) * 128], uT[:, kc, :],
                       kc == 0, kc == KC - 1, [("w", s)] + UT_ALL, bk)
                ACT(sgA, ps[bk], AF.Sigmoid, [], [("ps", bk), ("sgA",)])
                TT_(ypt[:, yb + blk, :], ypt[:, yb + blk, :], sgA, ALU.mult,
                    [("XN", yb + blk), ("sgA",)], [("XN", yb + blk)])
                yield
            w_release(s)

        def step3(cb):
            yb = (cb % 2) * 2
            s1 = w_acquire(f"wrb{t}_{cb}")
            s2 = w_acquire(f"gr{t}_{cb}")
            w1 = wview(s1, KC, 256)
            w2 = wview(s2, KC, 256)
            for blk in range(2):
                gblk = cb * 2 + blk
                bY, bG = bank(), bank()
                for kc in range(KC):
                    MM(ps[bY], w1[:, kc, blk * 128:(blk + 1) * 128], ogT[:, kc, :],
                       kc == 0, kc == KC - 1, [("w", s1)] + OGK_ALL, bY)
                for kc in range(KC):
                    MM(ps[bG], w2[:, kc, blk * 128:(blk + 1) * 128], uT[:, kc, :],
                       kc == 0, kc == KC - 1, [("w", s2)] + UT_ALL, bG)
                ACT(sgB, ps[bG], AF.Sigmoid, [], [("ps", bG), ("sgB",)])
                TT_(tprod, ps[bY], sgB, ALU.mult, [("sgB",)], [("ps", bY), ("tprod",)])
                TT_(mergedT[:, gblk, :], tprod, ypt[:, yb + blk, :], ALU.add, [("tprod",), ("XN", yb + blk)],
                    [("XM", gblk // 8)])
                yield
            w_release(s1)
            w_release(s2)

        yield from steps12(0)
        for cb in range(8):
            if cb + 1 < 8:
                yield from steps12(cb + 1)
            if cb == 0:
                drain(bg_holder[0])
                bg_holder[0] = None
            yield from step3(cb)

    def stage_C(xsrc, t):
        for sub in range(NSUB):
            SPDMA(hbuf[:, sub, :], xsrc[t * TT + sub * 128: t * TT + (sub + 1) * 128, :], f"h{sub}",
                  [], HK(sub))
        for cb in range(8):
            s = w_acquire(f"wo{t}_{cb}")
            w = wview(s, KC, 256)
            bks = [bank() for _ in range(NSUB)]
            for kc in range(KC):
                for sub in range(NSUB):
                    MM(ps[bks[sub]][:, 0:256], mergedT[:, kc, sub * 128:(sub + 1) * 128], w[:, kc, :],
                       kc == 0, kc == KC - 1, [("w", s), ("XM", kc // 8)], bks[sub])
            w_release(s)
            for sub in range(NSUB):
                hs = hbuf[:, sub, cb * 256:(cb + 1) * 256]
                TT_(hs, ps[bks[sub]][:, 0:256], hs, ALU.add, HK(sub), [("ps", bks[sub])] + HK(sub))

    def stage_D(t):
        drain(rms_to_T(lambda sub: hbuf[:, sub, :], lambda sub: HK(sub), g2c, D,
                       dst_bufs=[(xin[0], [("XM", 0)]), (xin[1], [("XM", 1)])]))

    def stage_EF(t, bgA=None):
        sgs = ((sgA, ("sgA",)), (sgB, ("sgB",)))
        for half in range(2):
            for jj in range(11):
                s = w_acquire(f"fia{t}_{half}_{jj}")
                w = wview(s, KC, 256)
                for jsub in range(2):
                    bA = bank()
                    for kc in range(KC):
                        MM(ps[bA], w[:, kc, jsub * 128:(jsub + 1) * 128], uT[:, kc, :],
                           kc == 0, kc == KC - 1, [("w", s)] + UT_ALL, bA)
                    ACT(sgs[jsub][0], ps[bA], AF.Silu, [], [("ps", bA), sgs[jsub][1]])
                w_release(s)
                s = w_acquire(f"fib{t}_{half}_{jj}")
                w = wview(s, KC, 256)
                for jsub in range(2):
                    jl = jj * 2 + jsub
                    bB = bank()
                    for kc in range(KC):
                        MM(ps[bB], w[:, kc, jsub * 128:(jsub + 1) * 128], uT[:, kc, :],
                           kc == 0, kc == KC - 1, [("w", s)] + UT_ALL, bB)
                    TT_(hffT[:, jl, :], sgs[jsub][0], ps[bB], ALU.mult, [sgs[jsub][1]],
                        [("ps", bB)] + HFK(jl))
                w_release(s)
            for cb in range(8):
                bks = [bank() for _ in range(NSUB)]
                for part, nk in (("a", 16), ("b", 6)):
                    s = w_acquire(f"fo{part}{t}_{half}_{cb}")
                    w = wview(s, nk, 256)
                    for kk in range(nk):
                        jl = kk if part == "a" else 16 + kk
                        for sub in range(NSUB):
                            MM(ps[bks[sub]][:, 0:256], hffT[:, jl, sub * 128:(sub + 1) * 128], w[:, kk, :],
                               jl == 0, jl == 21, [("w", s)] + HFK(jl), bks[sub])
                    w_release(s)
                for sub in range(NSUB):
                    hs = hbuf[:, sub, cb * 256:(cb + 1) * 256]
                    TT_(hs, ps[bks[sub]][:, 0:256], hs, ALU.add, HK(sub), [("ps", bks[sub])] + HK(sub))
                if half == 1 and bgA is not None and cb % 2 == 1:
                    next(bgA, None)
        drain(bgA)

    def stage_out(t):
        for sub in range(NSUB):
            src = hbuf[:, sub, :]
            ss, kss = scol()
            rs, krs = scol()
            ri, kri = scol()
            ACT(junk, src, AF.Square, HK(sub), [("junk",), kss], accum=ss)
            ACT(rs, ss, AF.Sqrt, [kss], [krs], scale=1.0 / D, bias=EPS)
            RECIP(ri, rs, [krs], [kri])
            STT(src, src, ri, gfbc, ALU.mult, ALU.mult, HK(sub) + [kri, ("gfbc",)], HK(sub))
            SPDMA(out_d[t * TT + sub * 128: t * TT + (sub + 1) * 128, :], src, f"o{sub}", HK(sub), [])

    nxt = stage_A(xp, 0, ub=0)
    for t in range(NT):
        drain(nxt)
        load_rot(c_rotp, t)
        nxt = stage_A(xp, t + 1, ub=(t + 1) % 2) if t + 1 < NT else stage_A(xm, 0, ub=0)
        bg = None
        for h in range(H):
            merge_streams(gen_proj_prefix(t, h), bg)
            bg = gen_ret(t, h, True, last_prefix_tile=(t == NT - 1))
            if h % 2 == 1:
                next(nxt, None)
        drain(bg)
        if t == NT - 1:
            pz_proj(t, [f"ppz{c4}" for c4 in range(4)], lambda sub: pzhalo, [("pzhalo",)], [NSUB - 1], ub=t % 2)
    nextA = nxt
    for t in range(NT):
        drain(nextA)
        load_rot(c_rotm, t)
        pool_chain(t)
        bg = None
        for h in range(H):
            merge_streams(gen_proj(t, h), bg)
            bg = gen_ret(t, h, False)
        holder = [bg]
        for _ in gen_merge(t, holder):
            if holder[0] is not None:
                if next(holder[0], "end") == "end":
                    holder[0] = None
        drain(holder[0])
        stage_C(xm, t)
        stage_D(t)
        nextA = stage_A(xm, t + 1) if t + 1 < NT else None
        stage_EF(t, nextA)
        nextA = None
        stage_out(t)
    assert wstate["next_acq"] == len(units), (wstate, len(units))

    P.finalize()
    sem_names = list(ENGS[:4]) + sorted(P.dma_cnt.keys())
    sems = {n: st.enter_context(nc.semaphore(n)) for n in sem_names}
    block = st.enter_context(nc.Block())

    @block.tensor
    def _(e):
        P.emit_engine("pe", e, sems, lookahead=0)

    @block.scalar
    def _(e):
        P.emit_engine("act", e, sems)

    @block.vector
    def _(e):
        P.emit_engine("dve", e, sems)

    @block.gpsimd
    def _(e):
        P.emit_engine("pool", e, sems)

    @block.sync
    def _(e):
        P.emit_engine("sp", e, sems)
        for n in ("o0", "o1", "o2", "o3"):
            e.wait_ge(sems[n], P.dma_cnt[n])

    st.close()
    return nc


def _const_tables(pos0_main, pos0_pre, first_is_seq_start):
    h = np.arange(H, dtype=np.float64)
    gam = 1.0 - 2.0 ** (-5.0 - h)
    n = np.arange(128, dtype=np.float64)
    m = n[:, None]
    nn = n[None, :]
    same = (m // 64) == (nn // 64)
    low = (m // 64 == 0) & (nn // 64 == 1)
    mask = np.zeros((H, 128, 128), np.float64)
    for i in range(H):
        mk = np.where(same, gam[i] ** np.abs(nn - m), 0.0) + np.where(low, gam[i] ** (nn - m), 0.0)
        mask[i] = mk * gam[i] ** (-(nn + 1.0))
    c_mask = np.ascontiguousarray(mask.transpose(1, 0, 2).reshape(128, H * 128)).astype(np.float32)
    kdec = (gam[None, :] ** (127.0 - n[:, None])).astype(np.float32)
    epsh = (EPS * gam[None, :] ** (-2.0 * (n[:, None] + 1.0))).astype(np.float32)

    def rot(pos0):
        inv_freq = (1.0 / (10000.0 ** (np.arange(0, 256, 2, dtype=np.float32) / np.float32(256)))).astype(np.float32)
        pos = (pos0 + np.arange(NTOK)).astype(np.float32)
        ang = (pos[None, :] * inv_freq[:, None]).astype(np.float32)
        r = np.stack([np.cos(ang), np.sin(ang)], axis=1).astype(np.float32)
        return np.ascontiguousarray(r * np.float32(1.0 / 16.0))

    ag = np.zeros((128, 12, 128), np.float32)
    s_ = np.arange(128)[:, None]
    t_ = np.arange(128)[None, :]
    for gi, w in enumerate((2, 4, 8, 16)):
        gen = ((s_ <= t_) & (s_ > t_ - w)).astype(np.float64) / w - (s_ == t_)
        halo = ((s_ - 128) > (t_ - w)).astype(np.float64) / w
        cnt = np.minimum(t_ + 1, w).astype(np.float64)
        first = ((s_ <= t_) & (s_ > t_ - w)).astype(np.float64) / cnt - (s_ == t_)
        ag[:, gi, :] = gen
        ag[:, 4 + gi, :] = halo
        ag[:, 8 + gi, :] = first if first_is_seq_start else gen
    return c_mask, kdec, epsh, rot(pos0_main), rot(pos0_pre), np.ascontiguousarray(ag.reshape(128, 12 * 128))


_NC_CACHE = {}


def kernel(x, norm1_g, w_in, w_ret_branch, w_pool_group, pool_scale, w_pool_branch,
           w_out, norm2_g, w_ffn_in, w_ffn_out, norm_final_g):
    f32 = np.float32
    x = np.asarray(x, f32)
    B, S, _ = x.shape
    ncores = 8
    if "nc" not in _NC_CACHE:
        _NC_CACHE["nc"] = build_nc()
    nc = _NC_CACHE["nc"]

    def colmajor(v, nblk):
        return np.ascontiguousarray(np.asarray(v, f32).reshape(nblk, 128).T)

    shared = {
        "w_in": np.ascontiguousarray(np.asarray(w_in, f32)[0]),
        "w_rb": np.ascontiguousarray(np.asarray(w_ret_branch, f32)[0]),
        "w_pg": np.ascontiguousarray(np.asarray(w_pool_group, f32)[0].reshape(1024, 256)),
        "w_pb": np.ascontiguousarray(np.asarray(w_pool_branch, f32)[0]),
        "w_o": np.ascontiguousarray(np.asarray(w_out, f32)[0]),
        "w_fi": np.ascontiguousarray(np.asarray(w_ffn_in, f32)[0]),
        "w_fo": np.ascontiguousarray(np.asarray(w_ffn_out, f32)[0]),
        "c_gf": np.ascontiguousarray(np.broadcast_to(np.asarray(norm_final_g, f32)[None, :], (128, D))),
        "c_ident": np.eye(128, dtype=f32),
    }
    g1c = colmajor(norm1_g[0], 16)
    g2c = colmajor(norm2_g[0], 16)
    psc = colmajor(pool_scale[0], 8)
    in_maps = []
    zeros = np.zeros((NTOK, D), f32)
    for c in range(ncores):
        b, half = c // 2, c % 2
        c_mask, kdec, epsh, rotm, rotp, ag = _const_tables(half * NTOK, 0, half == 0)
        small = np.zeros((128, 64), f32)
        small[:, 0:16] = g1c
        small[:, 16:32] = g2c
        small[:, 32:40] = psc
        small[:, 40:48] = kdec
        small[:, 48:56] = epsh
        m = dict(shared)
        m["xm"] = np.ascontiguousarray(x[b, half * NTOK:(half + 1) * NTOK])
        m["xp"] = np.ascontiguousarray(x[b, 0:NTOK]) if half == 1 else zeros
        m["c_small"] = small
        m["c_mask"] = c_mask
        m["c_ag"] = ag
        m["c_rotm"] = rotm
        m["c_rotp"] = rotp
        in_maps.append(m)
    res = run_bass_kernel_spmd(nc, in_maps, core_ids=list(range(ncores)))
    out = np.empty((B, S, D), f32)
    for c in range(ncores):
        b, half = c // 2, c % 2
        out[b, half * NTOK:(half + 1) * NTOK] = res.results[c]["out"]
    return out
```
